# Optimizing a Trainium2 kernel written in Bass

```python
import jax, jax.numpy as jnp
from jax import lax
import numpy as np

D_MODEL = 1024
BATCH = 1
SEQ = 16384
DEPTH = 4

CHUNK = 64
N_MIXERS = 2
N_A_LAYERS = (DEPTH + 1) // 2
N_B_LAYERS = DEPTH // 2
PLE_DIM = 256
FFN_HIDDEN = 2816
ML_HEADS = 4
ML_DQK = D_MODEL // 4
ML_DV = D_MODEL // 2
ML_INNER = ML_HEADS * ML_DV
ML_QK = ML_HEADS * ML_DQK
ML_CONV = 4
ML_IN_COLS = 2 * ML_QK + 2 * ML_INNER + 2 * ML_HEADS
GM_BLOCK = 128
GM_GROUPS = 8
GM_WIDTH = 2 * D_MODEL
GM_GDIM = GM_WIDTH // GM_GROUPS
DN_ALPHA = (2.0 * DEPTH) ** 0.25
DN_BETA = (8.0 * DEPTH) ** -0.25
LN_EPS = 1e-5

kernel_name = 'hybrid_mlstm_gmlp_macaron_deepnorm'


def layer_norm(x, g, b):
    x32 = x.astype(jnp.float32)
    mu = jnp.mean(x32, axis=-1, keepdims=True)
    var = jnp.mean(jnp.square(x32 - mu), axis=-1, keepdims=True)
    y = (x32 - mu) * lax.rsqrt(var + LN_EPS) * g.astype(jnp.float32) + b.astype(jnp.float32)
    return y.astype(x.dtype)


def swiglu_ffn(x, w_gu, w_d):
    g, u = jnp.split(x @ w_gu, 2, axis=-1)
    return (jax.nn.silu(g) * u) @ w_d


def causal_depthwise_conv(x, w):
    return lax.conv_general_dilated(
        x, w[:, None, :], window_strides=(1,), padding=[(w.shape[0] - 1, 0)],
        dimension_numbers=('NWC', 'WIO', 'NWC'), feature_group_count=x.shape[-1])


def mlstm_cell(q, k, v, ig, fg):
    B, S, H, DK = q.shape
    DV = v.shape[-1]
    NC = S // CHUNK

    def to_chunks(a):
        return a.reshape(B, NC, CHUNK, H, a.shape[-1]).transpose(1, 0, 3, 2, 4)

    def gate_chunks(a):
        return a.reshape(B, NC, CHUNK, H).transpose(1, 0, 3, 2)

    qc, kc, vc = to_chunks(q), to_chunks(k), to_chunks(v)
    igc = gate_chunks(ig)
    bc = jnp.cumsum(gate_chunks(jax.nn.log_sigmoid(fg)), axis=-1)
    mask = jnp.tril(jnp.ones((CHUNK, CHUNK), dtype=bool))

    def step(carry, xs):
        C, n, m = carry
        q_, k_, v_, i_, b_ = xs
        d_log = jnp.where(mask, b_[..., :, None] - b_[..., None, :] + i_[..., None, :], -jnp.inf)
        inter = b_ + m[..., None]
        m_t = jnp.maximum(inter, jnp.max(d_log, axis=-1))
        s_mat = jnp.einsum('bhtd,bhsd->bhts', q_, k_) * jnp.exp(d_log - m_t[..., None])
        w_inter = jnp.exp(inter - m_t)
        num = (jnp.einsum('bhts,bhsv->bhtv', s_mat, v_)
               + w_inter[..., None] * jnp.einsum('bhvd,bhtd->bhtv', C, q_))
        den = jnp.sum(s_mat, axis=-1) + w_inter * jnp.einsum('bhd,bhtd->bht', n, q_)
        h = num / jnp.maximum(jnp.abs(den), jnp.exp(-m_t))[..., None]
        b_last = b_[..., -1]
        w_log = b_last[..., None] - b_ + i_
        m_new = jnp.maximum(b_last + m, jnp.max(w_log, axis=-1))
        decay = jnp.exp(b_last + m - m_new)
        w_s = jnp.exp(w_log - m_new[..., None])
        C = decay[..., None, None] * C + jnp.einsum('bhs,bhsv,bhsd->bhvd', w_s, v_, k_)
        n = decay[..., None] * n + jnp.einsum('bhs,bhsd->bhd', w_s, k_)
        return (C, n, m_new), h

    init = (jnp.zeros((B, H, DV, DK), jnp.float32), jnp.zeros((B, H, DK), jnp.float32),
            jnp.zeros((B, H), jnp.float32))
    _, h = lax.scan(step, init, (qc, kc, vc, igc, bc))
    return h.transpose(1, 0, 3, 2, 4).reshape(B, S, H, DV)


def mlstm_mixer(x, w_in, b_in, conv_w, norm_g, w_out):
    B, S, _ = x.shape
    proj = x @ w_in + b_in
    qk, v, z, gates = jnp.split(proj, [2 * ML_QK, 2 * ML_QK + ML_INNER, 2 * ML_QK + 2 * ML_INNER], axis=-1)
    qk = jax.nn.silu(causal_depthwise_conv(qk, conv_w))
    q, k = jnp.split(qk, 2, axis=-1)
    q = q.reshape(B, S, ML_HEADS, ML_DQK).astype(jnp.float32)
    k = (k.reshape(B, S, ML_HEADS, ML_DQK) * (ML_DQK ** -0.5)).astype(jnp.float32)
    v = v.reshape(B, S, ML_HEADS, ML_DV).astype(jnp.float32)
    ig, fg = jnp.split(gates.astype(jnp.float32), 2, axis=-1)
    h = mlstm_cell(q, k, v, ig, fg)
    mu = jnp.mean(h, axis=-1, keepdims=True)
    var = jnp.mean(jnp.square(h - mu), axis=-1, keepdims=True)
    h = ((h - mu) * lax.rsqrt(var + LN_EPS)).reshape(B, S, ML_INNER) * norm_g.astype(jnp.float32)
    h = (h * jax.nn.sigmoid(z.astype(jnp.float32))).astype(x.dtype)
    return h @ w_out


def gmlp_mixer(x, w_in, b_in, vn_g, vn_b, ws, bs, w_out):
    B, S, _ = x.shape
    NB = S // GM_BLOCK
    u, v = jnp.split(jax.nn.gelu(x @ w_in + b_in, approximate=False), 2, axis=-1)
    v = layer_norm(v, vn_g, vn_b).reshape(B, NB, GM_BLOCK, GM_GROUPS, GM_GDIM)
    blk = jnp.arange(GM_BLOCK) // CHUNK
    mask = blk[:, None] >= blk[None, :]
    ws_m = jnp.where(mask, ws, jnp.zeros((), ws.dtype))
    s = jnp.einsum('gts,bnsgc->bntgc', ws_m, v) + bs.T[:, :, None]
    return (u * s.reshape(B, S, GM_WIDTH)) @ w_out


def setup_inputs(seed: int = 0) -> dict:
    key = jax.random.key(seed)
    ks = jax.random.split(key, 32)
    nrm = lambda k, shape: jax.random.normal(k, shape, jnp.float32)
    D, F = D_MODEL, FFN_HIDDEN
    ml_b_in = 0.02 * nrm(ks[8], (N_A_LAYERS, ML_IN_COLS))
    ml_b_in = ml_b_in.at[:, -ML_HEADS:].add(jnp.linspace(3.0, 6.0, ML_HEADS, dtype=jnp.float32))
    return {
        'x': nrm(ks[0], (BATCH, SEQ, D)),
        'p': nrm(ks[1], (DEPTH, BATCH, SEQ, PLE_DIM)),
        'ln_g': 1.0 + 0.02 * nrm(ks[2], (DEPTH, 4, D)),
        'ln_b': 0.02 * nrm(ks[3], (DEPTH, 4, D)),
        'ffn1_wgu': nrm(ks[4], (DEPTH, D, 2 * F)) * D ** -0.5,
        'ffn1_wd': nrm(ks[5], (DEPTH, F, D)) * (F ** -0.5 * DN_BETA),
        'ffn2_wgu': nrm(ks[6], (DEPTH, D, 2 * F)) * D ** -0.5,
        'ffn2_wd': nrm(ks[7], (DEPTH, F, D)) * (F ** -0.5 * DN_BETA),
        'ml_w_in': nrm(ks[9], (N_A_LAYERS, D, ML_IN_COLS)) * D ** -0.5,
        'ml_b_in': ml_b_in,
        'ml_conv': nrm(ks[10], (N_A_LAYERS, ML_CONV, 2 * ML_QK)) * ML_CONV ** -0.5,
        'ml_norm_g': 1.0 + 0.02 * nrm(ks[11], (N_A_LAYERS, ML_INNER)),
        'ml_w_out': nrm(ks[12], (N_A_LAYERS, ML_INNER, D)) * (ML_INNER ** -0.5 * DN_BETA),
        'gm_w_in': nrm(ks[13], (N_B_LAYERS, D, 2 * GM_WIDTH)) * D ** -0.5,
        'gm_b_in': 0.02 * nrm(ks[14], (N_B_LAYERS, 2 * GM_WIDTH)),
        'gm_vn_g': 1.0 + 0.02 * nrm(ks[15], (N_B_LAYERS, GM_WIDTH)),
        'gm_vn_b': 0.02 * nrm(ks[16], (N_B_LAYERS, GM_WIDTH)),
        'gm_ws': nrm(ks[17], (N_B_LAYERS, GM_GROUPS, GM_BLOCK, GM_BLOCK)) * GM_BLOCK ** -0.5,
        'gm_bs': 1.0 + 0.02 * nrm(ks[18], (N_B_LAYERS, GM_GROUPS, GM_BLOCK)),
        'gm_w_out': nrm(ks[19], (N_B_LAYERS, GM_WIDTH, D)) * (GM_WIDTH ** -0.5 * DN_BETA),
        'ple_wp': nrm(ks[20], (DEPTH, PLE_DIM, D)) * (PLE_DIM ** -0.5 * DN_BETA),
        'ple_wg': nrm(ks[21], (DEPTH, D, D)) * D ** -0.5,
        'ple_bg': 0.02 * nrm(ks[22], (DEPTH, D)),
    }


def reference(x, p, ln_g, ln_b, ffn1_wgu, ffn1_wd, ffn2_wgu, ffn2_wd,
              ml_w_in, ml_b_in, ml_conv, ml_norm_g, ml_w_out,
              gm_w_in, gm_b_in, gm_vn_g, gm_vn_b, gm_ws, gm_bs, gm_w_out,
              ple_wp, ple_wg, ple_bg):
    for i in range(DEPTH):
        x = layer_norm(DN_ALPHA * x + 0.5 * swiglu_ffn(x, ffn1_wgu[i], ffn1_wd[i]), ln_g[i, 0], ln_b[i, 0])
        j = i // N_MIXERS
        if i % N_MIXERS == 0:
            mix = mlstm_mixer(x, ml_w_in[j], ml_b_in[j], ml_conv[j], ml_norm_g[j], ml_w_out[j])
        else:
            mix = gmlp_mixer(x, gm_w_in[j], gm_b_in[j], gm_vn_g[j], gm_vn_b[j], gm_ws[j], gm_bs[j], gm_w_out[j])
        x = layer_norm(DN_ALPHA * x + mix, ln_g[i, 1], ln_b[i, 1])
        x = layer_norm(DN_ALPHA * x + 0.5 * swiglu_ffn(x, ffn2_wgu[i], ffn2_wd[i]), ln_g[i, 2], ln_b[i, 2])
        ple = jax.nn.sigmoid(x @ ple_wg[i] + ple_bg[i]) * (p[i] @ ple_wp[i])
        x = layer_norm(DN_ALPHA * x + ple, ln_g[i, 3], ln_b[i, 3])
    return x
```

```python
from contextlib import ExitStack
import numpy as np
import concourse.bass as bass
import concourse.mybir as mybir
from concourse.bass_utils import run_bass_kernel_spmd

F32 = mybir.dt.float32
BF16 = mybir.dt.bfloat16
AF = mybir.ActivationFunctionType
ALU = mybir.AluOpType

NCORES = 8
D = 1024
DC = 8
SEQ = 16384
T = SEQ // NCORES
TT = 512
NTT = T // TT
DEPTH = 4
FF = 2816
FC = FF // 128
GF = 2
PLE = 256
ALPHA = (2.0 * DEPTH) ** 0.25
EPS = 1e-5
SLOT = 6144
NSLOT = 4
LW = 256


class Buf:
    __slots__ = ("name", "w", "r", "sem", "cnt")

    def __init__(self, name):
        self.name = name
        self.w = None
        self.r = {}
        self.sem = None
        self.cnt = 0


class Op:
    __slots__ = ("eng", "fn", "deps", "idx", "inc", "sig", "dma", "sembuf")


ENGS = ("pe", "act", "dve", "pool", "sp")


class Plan:
    def __init__(self):
        self.ops = {e: [] for e in ENGS}
        self.bufs = []

    def buf(self, name):
        b = Buf(name)
        self.bufs.append(b)
        return b

    def add(self, eng, fn, reads=(), writes=(), dma=False, sembuf=None):
        op = Op()
        op.eng, op.fn, op.dma, op.inc, op.sig = eng, fn, dma, False, None
        op.idx = len(self.ops[eng])
        op.sembuf = None
        if dma:
            op.sembuf = sembuf if sembuf is not None else writes[0]
        deps = {}
        for b in reads:
            if b.w is not None:
                deps[b.w] = True
        for b in writes:
            if b.w is not None:
                deps[b.w] = True
            for r in b.r.values():
                deps.setdefault(r, False)
        fd = []
        for d, hard in deps.items():
            if d is op:
                continue
            if (not d.dma) and (not dma) and d.eng == eng and (eng == "pe" or not hard):
                continue
            if d.dma and dma and d.eng == eng and d.sembuf is op.sembuf:
                continue
            d.inc = True
            fd.append(d)
        op.deps = fd
        key = ("dma", op.sembuf.name) if dma else eng
        for b in reads:
            b.r[key] = op
        for b in writes:
            b.w = op
            b.r = {}
        self.ops[eng].append(op)
        return op

    def emit(self, nc, stack):
        engsem = {e: stack.enter_context(nc.semaphore("s_" + e)) for e in ENGS}
        for b in self.bufs:
            b.cnt = 0
        for e in ENGS:
            seq = 0
            for op in self.ops[e]:
                if op.dma:
                    sb = op.sembuf
                    if sb.sem is None:
                        sb.sem = stack.enter_context(nc.semaphore("d_" + sb.name))
                    sb.cnt += 16
                    op.sig = (sb.sem, sb.cnt)
                elif op.inc:
                    seq += 1
                    op.sig = (engsem[e], seq)
        block = stack.enter_context(nc.Block())
        plan = self

        def run(e):
            def body(eng):
                waited = {}
                for op in plan.ops[e]:
                    need = {}
                    for d in op.deps:
                        s, v = d.sig
                        k = id(s)
                        if k not in need or need[k][1] < v:
                            need[k] = (s, v)
                    for k, (s, v) in need.items():
                        if waited.get(k, 0) < v:
                            eng.wait_ge(s, v)
                            waited[k] = v
                    if op.fn is None:
                        continue
                    ins = op.fn(eng)
                    if op.dma:
                        ins.then_inc(op.sig[0], 16)
                    elif op.inc:
                        ins.then_inc(op.sig[0], 1)
            return body

        block.tensor(run("pe"))
        block.scalar(run("act"))
        block.vector(run("dve"))
        block.gpsimd(run("pool"))
        block.sync(run("sp"))


class Builder:
    def __init__(self):
        self.nc = bass.Bass("TRN2", target_bir_lowering=False)
        self.P = Plan()
        self.stack = ExitStack()

    def dram_in(self, name, shape, dtype=F32):
        return self.nc.dram_tensor(name, list(shape), dtype, kind="ExternalInput").ap()

    def sb(self, name, shape, dtype):
        return self.stack.enter_context(self.nc.sbuf_tensor("s_" + name, list(shape), dtype))

    def ps(self, name):
        return self.stack.enter_context(self.nc.psum_tensor("p_" + name, [128, 512], F32))

    def mm(self, out, lhsT, rhs, start, stop, reads, writes):
        self.P.add("pe", lambda e: e.matmul(out, lhsT, rhs, start=start, stop=stop), reads, writes)

    def act(self, out, in_, func, reads, writes, bias=None, scale=None):
        kw = {}
        if bias is not None:
            kw["bias"] = bias
        if scale is not None:
            kw["scale"] = scale
        self.P.add("act", lambda e: e.activation(out, in_, func, **kw), reads, writes)

    def tt(self, eng, out, in0, in1, op, reads, writes):
        self.P.add(eng, lambda e: e.tensor_tensor(out, in0, in1, op), reads, writes)

    def ts(self, eng, out, in0, s1, s2, op0, op1, reads, writes):
        if op1 is None:
            self.P.add(eng, lambda e: e.tensor_scalar(out, in0, s1, None, op0), reads, writes)
        else:
            self.P.add(eng, lambda e: e.tensor_scalar(out, in0, s1, s2, op0, op1), reads, writes)

    def stt(self, out, in0, scalar, in1, op0, op1, reads, writes):
        self.P.add("dve", lambda e: e.scalar_tensor_tensor(out, in0, scalar, in1, op0, op1), reads, writes)

    def dma(self, q, out, in_, reads, writes, sembuf=None):
        self.P.add(q, lambda e: e.dma_start(out=out, in_=in_), reads, writes, dma=True, sembuf=sembuf)

    def ring_init(self):
        self.slots = [self.sb("wslot%d" % i, [128, SLOT], BF16) for i in range(NSLOT)]
        self.slot_bufs = [self.P.buf("wslot%d" % i) for i in range(NSLOT)]
        self.wqueue = []
        self.w_loaded = 0
        self.w_used = 0
        self.dram_w = self.P.buf("dram_w")

    def ring_schedule(self, pieces):
        self.wqueue.append(pieces)

    def _ring_issue(self):
        i = self.w_loaded
        s = i % NSLOT
        for (off, kc, cols, src) in self.wqueue[i]:
            dst = self.slots[s][:, off:off + kc * cols].rearrange("p (k c) -> p k c", k=kc)
            self.dma("pool", dst, src, [self.dram_w], [self.slot_bufs[s]])
        self.w_loaded += 1

    def ring_prime(self):
        while self.w_loaded < min(NSLOT - 1, len(self.wqueue)):
            self._ring_issue()

    def ring_next(self):
        i = self.w_used
        assert i < self.w_loaded, "weight group not issued"
        self.w_used += 1
        while self.w_loaded < len(self.wqueue) and self.w_loaded < i + NSLOT:
            self._ring_issue()
        return self.slots[i % NSLOT], self.slot_bufs[i % NSLOT]

    def common(self, nslot):
        P = self.P
        global NSLOT
        NSLOT = nslot
        self.dram_b = P.buf("dram_in")
        self.ones = self.sb("ones", [128, 128], BF16)
        self.onesb = P.buf("ones")
        P.add("pool", lambda e: e.memset(self.ones[:, :], 1.0), [], [self.onesb])
        self.epsb = self.sb("epsb", [128, 1], F32)
        self.epsbb = P.buf("epsb")
        P.add("pool", lambda e: e.memset(self.epsb[:, :], EPS), [], [self.epsbb])
        NB = DEPTH * 4 * DC
        lnp = self.dram_in("lnp", [128, 2 * NB])
        self.lng = self.sb("lng", [128, 2 * NB], F32)
        self.lnga = self.sb("lnga", [128, 2 * NB], F32)
        self.lngb = P.buf("lng")
        self.dma("sp", self.lng[:, :], lnp, [self.dram_b], [self.lngb])
        P.add("dve", lambda e: e.tensor_scalar(self.lnga[:, :], self.lng[:, :], ALPHA, None, ALU.mult),
              [self.lngb], [self.lngb])
        self.zb = self.sb("zb", [128, DC, LW], BF16)
        self.zbb = P.buf("zb")
        self.zq = self.sb("zq", [128, DC, LW], BF16)
        self.zqb = P.buf("zq")
        self.st = [self.sb("st%d" % i, [128, LW], F32) for i in range(4)]
        self.stb = [P.buf("st%d" % i) for i in range(4)]
        self.sg = [self.sb("sg%d" % i, [128, TT], F32) for i in range(2)]
        self.sgb = [P.buf("sg%d" % i) for i in range(2)]
        self.pb = [self.ps("pb%d" % i) for i in range(8)]
        self.pbb = [P.buf("pb%d" % i) for i in range(8)]
        self.ring_init()

    def finish(self, outbufs):
        self.P.add("sp", None, outbufs, [])
        self.P.emit(self.nc, self.stack)
        self.stack.close()
        return self.nc

    def layernorm(self, l, k, Xv, XBv, xb, xbb, last):
        for hf in range(TT // LW):
            hs = slice(hf * LW, (hf + 1) * LW)
            self._ln(l, k, Xv[:, :, hs], None if XBv is None else XBv[:, :, hs], xb, xbb, last)

    def _ln(self, l, k, Xs, XBs, xb, xbb, last):
        S1, S2 = self.pb[0], self.pb[1]
        s1b, s2b = self.pbb[0], self.pbb[1]
        self.act(self.zb[:, :, :], Xs, AF.Copy, [xb], [self.zbb])
        self.act(self.zq[:, :, :], Xs, AF.Square, [xb], [self.zqb])
        for c in range(DC):
            self.mm(S1[:, 0:LW], self.ones[:, :], self.zb[:, c, :], c == 0, c == DC - 1, [self.onesb, self.zbb], [s1b])
        for c in range(DC):
            self.mm(S2[:, 0:LW], self.ones[:, :], self.zq[:, c, :], c == 0, c == DC - 1, [self.onesb, self.zqb], [s2b])
        mean, var, rstd, mr = self.st
        mb, vb, rb, mrb = self.stb
        self.ts("dve", mean[:, :], S1[:, 0:LW], 1.0 / D, None, ALU.mult, None, [s1b], [mb])
        self.tt("dve", var[:, :], mean[:, :], mean[:, :], ALU.mult, [mb], [vb])
        self.stt(var[:, :], S2[:, 0:LW], 1.0 / D, var[:, :], ALU.mult, ALU.subtract, [s2b, vb], [vb])
        self.act(var[:, :], var[:, :], AF.Sqrt, [vb, self.epsbb], [vb], bias=self.epsb[:, :])
        self.P.add("dve", lambda e: e.reciprocal(rstd[:, :], var[:, :]), [vb], [rb])
        self.tt("dve", mr[:, :], mean[:, :], rstd[:, :], ALU.mult, [mb, rb], [mrb])
        col = (l * 4 + k) * DC
        NB = DEPTH * 4 * DC
        for c in range(DC):
            self.tt("dve", Xs[:, c, :], Xs[:, c, :], rstd[:, :], ALU.mult, [xb, rb], [xb])
            self.tt("pool", Xs[:, c, :], Xs[:, c, :], mr[:, :], ALU.subtract, [xb, mrb], [xb])
            gcol = self.lng[:, col + c:col + c + 1]
            bcol = self.lng[:, NB + col + c:NB + col + c + 1]
            if XBs is not None:
                self.act(XBs[:, c, :], Xs[:, c, :], AF.Identity, [xb, self.lngb], [xbb], bias=bcol, scale=gcol)
            if not last:
                gcol = self.lnga[:, col + c:col + c + 1]
                bcol = self.lnga[:, NB + col + c:NB + col + c + 1]
            self.act(Xs[:, c, :], Xs[:, c, :], AF.Identity, [xb, self.lngb], [xb], bias=bcol, scale=gcol)

    def build_stages(self, items, raw_in, final):
        nc, P = self.nc, self.P
        has_gm = any(s == "gm" for _, s in items)
        self.common(3 if has_gm else 4)
        xT = self.dram_in("xT", [D, T])
        out = nc.dram_tensor("outT", [D, T], F32, kind="ExternalOutput").ap()
        self.w = {}
        for (l, s) in items:
            if s in ("ffn1", "ffn2"):
                self.w[(s + "_wgu", l)] = self.dram_in("%s_wgu_%d" % (s, l), [D, 2 * FF])
                self.w[(s + "_wd", l)] = self.dram_in("%s_wd_%d" % (s, l), [FF, D])
            elif s == "ple":
                self.w[("ple_wp", l)] = self.dram_in("ple_wp_%d" % l, [PLE, D])
                self.w[("ple_wg", l)] = self.dram_in("ple_wg_%d" % l, [D, D])
                self.w[("pT", l)] = self.dram_in("pT_%d" % l, [PLE, T])
            elif s == "gm":
                self.w[("gm_w_in", l)] = self.dram_in("gm_w_in_%d" % l, [D, 4096])
                self.w[("gm_w_out", l)] = self.dram_in("gm_w_out_%d" % l, [2048, D])
                self.w[("gm_fm", l)] = self.dram_in("gm_fm_%d" % l, [128, 48])
                self.w[("gm_b_in", l)] = self.dram_in("gm_b_in_%d" % l, [1, 4096])
                self.w[("gm_ws", l)] = self.dram_in("gm_ws_%d" % l, [8, 128, 128])
                self.w[("gm_bs", l)] = self.dram_in("gm_bs_%d" % l, [128, 8 * 128])
        plb = self.dram_in("plb", [128, DEPTH * DC])
        self.X = self.sb("X", [128, DC, T], F32)
        self.XB = self.sb("XB", [128, DC, T], BF16)
        self.Xb = [P.buf("X%d" % i) for i in range(NTT)]
        self.XBb = [P.buf("XB%d" % i) for i in range(NTT)]
        self.plbs = self.sb("plbs", [128, DEPTH * DC], F32)
        self.plbb = P.buf("plb")
        self.dma("sp", self.plbs[:, :], plb, [self.dram_b], [self.plbb])
        self.hb = [self.sb("hb%d" % i, [128, GF, TT], BF16) for i in range(2)]
        self.hbb = [P.buf("hb%d" % i) for i in range(2)]
        if has_gm:
            self.gm_alloc()
            self.PT = self.V[:, 0:2, :]
            self.PTb = self.Vb
        else:
            self.PT = self.sb("PT", [128, 2, T], BF16)
            self.PTb = P.buf("PT")
        for (l, s) in items:
            self.schedule_weights(l, s)
        for tt in range(NTT):
            sl = slice(tt * TT, (tt + 1) * TT)
            self.dma("sp", self.X[:, :, sl], xT[:, sl].rearrange("(c p) t -> p c t", p=128),
                     [self.dram_b], [self.Xb[tt]])
        self.ring_prime()
        for tt in range(NTT):
            sl = slice(tt * TT, (tt + 1) * TT)
            if raw_in:
                self.act(self.XB[:, :, sl], self.X[:, :, sl], AF.Copy, [self.Xb[tt]], [self.XBb[tt]])
                P.add("dve", lambda e, sl=sl: e.tensor_scalar(self.X[:, :, sl], self.X[:, :, sl], ALPHA, None, ALU.mult),
                      [self.Xb[tt]], [self.Xb[tt]])
            else:
                self.act(self.XB[:, :, sl], self.X[:, :, sl], AF.Copy, [self.Xb[tt]], [self.XBb[tt]], scale=1.0 / ALPHA)
        for idx, (l, s) in enumerate(items):
            last = final and idx == len(items) - 1
            if s in ("ffn1", "ffn2"):
                self.ffn(l, s, last)
            elif s == "ple":
                self.ple(l, last)
            elif s == "gm":
                self.gm(l, last)
        outb = P.buf("out")
        for tt in range(NTT):
            sl = slice(tt * TT, (tt + 1) * TT)
            self.dma("sp", out[:, sl].rearrange("(c p) t -> p c t", p=128), self.X[:, :, sl],
                     [self.Xb[tt]], [outb], sembuf=outb)
        return self.finish([outb])

    def schedule_weights(self, l, st):
        if st in ("ffn1", "ffn2"):
            wgu = self.w[(st + "_wgu", l)].rearrange("(k p) c -> p k c", p=128)
            wd = self.w[(st + "_wd", l)].rearrange("(k p) c -> p k c", p=128)
            for g in range(FC // GF):
                c0 = g * GF * 128
                self.ring_schedule([
                    (0, DC, GF * 128, wgu[:, :, c0:c0 + GF * 128]),
                    (DC * GF * 128, DC, GF * 128, wgu[:, :, FF + c0:FF + c0 + GF * 128]),
                    (2 * DC * GF * 128, GF, D, wd[:, g * GF:(g + 1) * GF, :]),
                ])
        elif st == "ple":
            wg = self.w[("ple_wg", l)].rearrange("(k p) c -> p k c", p=128)
            wp = self.w[("ple_wp", l)].rearrange("(k p) c -> p k c", p=128)
            for h in range(2):
                self.ring_schedule([
                    (0, DC, 512, wg[:, :, h * 512:(h + 1) * 512]),
                    (DC * 512, 2, 512, wp[:, :, h * 512:(h + 1) * 512]),
                ])
        elif st == "gm":
            wi = self.w[("gm_w_in", l)].rearrange("(k p) c -> p k c", p=128)
            wo = self.w[("gm_w_out", l)].rearrange("(k p) c -> p k c", p=128)
            for tb in range(NTT):
                for g in range(8):
                    self.ring_schedule([(0, DC, 512, wi[:, :, g * 512:(g + 1) * 512])])
                for (d0, nd) in ((0, 3), (3, 3), (6, 2)):
                    self.ring_schedule([(0, 16, nd * 128, wo[:, :, d0 * 128:(d0 + nd) * 128])])

    def ffn(self, l, st, last):
        k = 0 if st == "ffn1" else 2
        ng = FC // GF
        cnt = 0
        for g in range(ng):
            slot, sbuf = self.ring_next()
            wg_ = slot[:, 0:DC * GF * 128].rearrange("p (k c) -> p k c", k=DC)
            wu_ = slot[:, DC * GF * 128:2 * DC * GF * 128].rearrange("p (k c) -> p k c", k=DC)
            wd_ = slot[:, 2 * DC * GF * 128:2 * DC * GF * 128 + GF * D].rearrange("p (k c) -> p k c", k=GF)
            for tt in range(NTT):
                sl = slice(tt * TT, (tt + 1) * TT)
                hbi = (g * NTT + tt) % 2
                for j in range(GF):
                    pg, pu = cnt % 2, 2 + cnt % 2
                    sgi = cnt % 2
                    cnt += 1
                    for kc in range(DC):
                        self.mm(self.pb[pg][:, :], wg_[:, kc, j * 128:(j + 1) * 128], self.XB[:, kc, sl],
                                kc == 0, kc == DC - 1, [sbuf, self.XBb[tt]], [self.pbb[pg]])
                    for kc in range(DC):
                        self.mm(self.pb[pu][:, :], wu_[:, kc, j * 128:(j + 1) * 128], self.XB[:, kc, sl],
                                kc == 0, kc == DC - 1, [sbuf, self.XBb[tt]], [self.pbb[pu]])
                    self.act(self.sg[sgi][:, :], self.pb[pg][:, :], AF.Silu, [self.pbb[pg]], [self.sgb[sgi]])
                    self.tt("dve", self.hb[hbi][:, j, :], self.sg[sgi][:, :], self.pb[pu][:, :], ALU.mult,
                            [self.sgb[sgi], self.pbb[pu]], [self.hbb[hbi]])
                for dc in range(DC):
                    py = 4 + dc % 4
                    for j in range(GF):
                        self.mm(self.pb[py][:, :], wd_[:, j, dc * 128:(dc + 1) * 128], self.hb[hbi][:, j, :],
                                j == 0, j == GF - 1, [sbuf, self.hbb[hbi]], [self.pbb[py]])
                    self.stt(self.X[:, dc, sl], self.pb[py][:, :], 0.5, self.X[:, dc, sl], ALU.mult, ALU.add,
                             [self.pbb[py], self.Xb[tt]], [self.Xb[tt]])
                if g == ng - 1:
                    self.layernorm(l, k, self.X[:, :, sl], self.XB[:, :, sl], self.Xb[tt], self.XBb[tt], last)

    def ple(self, l, last):
        self.dma("pool", self.PT[:, :, :], self.w[("pT", l)].rearrange("(k p) t -> p k t", p=128),
                 [self.dram_b], [self.PTb])
        cnt = 0
        for h in range(2):
            slot, sbuf = self.ring_next()
            wg_ = slot[:, 0:DC * 512].rearrange("p (k c) -> p k c", k=DC)
            wp_ = slot[:, DC * 512:DC * 512 + 2 * 512].rearrange("p (k c) -> p k c", k=2)
            for tt in range(NTT):
                sl = slice(tt * TT, (tt + 1) * TT)
                for j in range(4):
                    dc = h * 4 + j
                    pg, pp = cnt % 2, 2 + cnt % 2
                    sgi = cnt % 2
                    cnt += 1
                    for kc in range(DC):
                        self.mm(self.pb[pg][:, :], wg_[:, kc, j * 128:(j + 1) * 128], self.XB[:, kc, sl],
                                kc == 0, kc == DC - 1, [sbuf, self.XBb[tt]], [self.pbb[pg]])
                    for kc in range(2):
                        self.mm(self.pb[pp][:, :], wp_[:, kc, j * 128:(j + 1) * 128], self.PT[:, kc, sl],
                                kc == 0, kc == 1, [sbuf, self.PTb], [self.pbb[pp]])
                    self.act(self.sg[sgi][:, :], self.pb[pg][:, :], AF.Sigmoid, [self.pbb[pg], self.plbb], [self.sgb[sgi]],
                             bias=self.plbs[:, l * DC + dc:l * DC + dc + 1])
                    self.tt("dve", self.sg[sgi][:, :], self.sg[sgi][:, :], self.pb[pp][:, :], ALU.mult,
                            [self.sgb[sgi], self.pbb[pp]], [self.sgb[sgi]])
                    self.tt("pool", self.X[:, dc, sl], self.X[:, dc, sl], self.sg[sgi][:, :], ALU.add,
                            [self.sgb[sgi], self.Xb[tt]], [self.Xb[tt]])
                if h == 1:
                    self.layernorm(l, 3, self.X[:, :, sl], self.XB[:, :, sl], self.Xb[tt], self.XBb[tt], last)

    def gm_alloc(self):
        P = self.P
        self.U = self.sb("gU", [128, 16, TT], BF16)
        self.Ub = P.buf("gU")
        self.V = self.sb("gV", [128, 4, 2048], BF16)
        self.Vb = P.buf("gV")
        self.R = self.sb("gR", [128, 16, 128], F32)
        self.Rb = P.buf("gR")
        self.wsT = self.sb("gwsT", [128, 8, 128], BF16)
        self.wsTb = P.buf("gwsT")
        self.wsraw = self.sb("gwsraw", [128, 128], F32)
        self.wsrawb = P.buf("gwsraw")
        self.wstmp = self.sb("gwstmp", [128, 128], F32)
        self.wstmpb = P.buf("gwstmp")
        self.identf = self.sb("identf", [128, 128], F32)
        self.identfb = P.buf("identf")
        self.onesF = self.sb("onesF", [128, 128], F32)
        self.onesFb = P.buf("onesF")
        self.gfm = self.sb("gfm", [128, 48], F32)
        self.gfmb = P.buf("gfm")
        self.gbrowA = self.sb("gbrowA", [65, 512], BF16)
        self.gbrowB = self.sb("gbrowB", [1, 512], BF16)
        self.gbrowb = P.buf("gbrow")
        self.BSB = self.sb("gBSB", [128, 8, 128], F32)
        self.BSBb = P.buf("gBSB")
        self.gst = self.sb("gst", [128, 4, 4, 6], F32)
        self.gstb = P.buf("gst")
        self.gmv = self.sb("gmv", [128, 4, 2], F32)
        self.gmvb = P.buf("gmv")
        self.gsc = self.sb("gsc", [128, 4, 2], F32)
        self.gscb = P.buf("gsc")
        ident = self.dram_in("identf", [128, 128])
        self.dma("sp", self.identf[:, :], ident, [self.dram_b], [self.identfb])
        P.add("pool", lambda e: e.memset(self.onesF[:, :], 1.0), [], [self.onesFb])

    def gm(self, l, last):
        P = self.P
        self.dma("sp", self.gfm[:, :], self.w[("gm_fm", l)], [self.dram_b], [self.gfmb])
        for g in range(3):
            self.dma("pool", self.gbrowA[32 * g:32 * g + 1, :], self.w[("gm_b_in", l)][:, 2048 + g * 512:2048 + (g + 1) * 512],
                     [self.dram_b], [self.gbrowb])
        self.dma("pool", self.gbrowB[0:1, :], self.w[("gm_b_in", l)][:, 2048 + 3 * 512:4096], [self.dram_b], [self.gbrowb])
        self.dma("sp", self.BSB[:, :, :], self.w[("gm_bs", l)].rearrange("p (g t) -> p g t", g=8), [self.dram_b], [self.BSBb])
        T0 = self.pb[2]
        t0b = self.pbb[2]
        for g in range(8):
            self.dma("sp", self.wsraw[:, :], self.w[("gm_ws", l)][g], [self.dram_b], [self.wsrawb])
            self.mm(T0[:, 0:128], self.wsraw[:, :], self.identf[:, :], True, True, [self.wsrawb, self.identfb], [t0b])
            self.act(self.wstmp[:, :], T0[:, 0:128], AF.Copy, [t0b], [self.wstmpb])
            P.add("pool", lambda e: e.memset(self.wstmp[64:128, 0:64], 0.0), [self.wstmpb], [self.wstmpb])
            self.act(self.wsT[:, g, :], self.wstmp[:, :], AF.Copy, [self.wstmpb], [self.wsTb])
            self.mm(T0[:, 128:256], self.onesF[:, :], self.wstmp[:, :], True, True, [self.onesFb, self.wstmpb], [t0b])
            for fc in (2 * g, 2 * g + 1):
                self.stt(self.R[:, fc, :], T0[:, 128:256], self.gfm[:, 32 + fc:33 + fc], self.BSB[:, g, :], ALU.mult, ALU.add,
                         [t0b, self.gfmb, self.BSBb], [self.Rb])
        cnt = 0
        for tb in range(NTT):
            sl = slice(tb * TT, (tb + 1) * TT)
            xb, xbb = self.Xb[tb], self.XBb[tb]
            for g in range(4):
                slot, sbuf = self.ring_next()
                w_ = slot[:, 0:DC * 512].rearrange("p (k c) -> p k c", k=DC)
                for j in range(4):
                    fc = g * 4 + j
                    pg = cnt % 2
                    cnt += 1
                    for kc in range(DC):
                        self.mm(self.pb[pg][:, :], w_[:, kc, j * 128:(j + 1) * 128], self.XB[:, kc, sl],
                                kc == 0, kc == DC - 1, [sbuf, xbb], [self.pbb[pg]])
                    self.act(self.U[:, fc, :], self.pb[pg][:, :], AF.Gelu, [self.pbb[pg], self.gfmb], [self.Ub],
                             bias=self.gfm[:, fc:fc + 1])
            for g in range(4):
                slot, sbuf = self.ring_next()
                w_ = slot[:, 0:DC * 512].rearrange("p (k c) -> p k c", k=DC)
                for sub in range(4):
                    ts_ = slice(tb * TT + sub * 128, tb * TT + (sub + 1) * 128)
                    pg = cnt % 2
                    cnt += 1
                    for kc in range(DC):
                        self.mm(self.pb[pg][:, :], self.XB[:, kc, ts_], w_[:, kc, :], kc == 0, False,
                                [sbuf, xbb], [self.pbb[pg]])
                    if g < 3:
                        self.mm(self.pb[pg][:, :], self.ones[32 * g:32 * g + 1, :], self.gbrowA[32 * g:32 * g + 1, :], False, True,
                                [self.onesb, self.gbrowb], [self.pbb[pg]])
                    else:
                        self.mm(self.pb[pg][:, :], self.ones[0:1, :], self.gbrowB[0:1, :], False, True,
                                [self.onesb, self.gbrowb], [self.pbb[pg]])
                    self.act(self.V[:, sub, g * 512:(g + 1) * 512], self.pb[pg][:, :], AF.Gelu, [self.pbb[pg]], [self.Vb])
                    P.add("dve", lambda e, sub=sub, g=g: e.bn_stats(self.gst[:, sub, g, :], self.V[:, sub, g * 512:(g + 1) * 512]),
                          [self.Vb], [self.gstb])
            for sub in range(4):
                P.add("dve", lambda e, sub=sub: e.bn_aggr(self.gmv[:, sub, :], self.gst[:, sub, :, :]), [self.gstb], [self.gmvb])
                self.act(self.gsc[:, sub, 0:1], self.gmv[:, sub, 1:2], AF.Sqrt, [self.gmvb, self.epsbb], [self.gscb], bias=self.epsb[:, :])
                P.add("dve", lambda e, sub=sub: e.reciprocal(self.gsc[:, sub, 0:1], self.gsc[:, sub, 0:1]), [self.gscb], [self.gscb])
                self.stt(self.gsc[:, sub, 1:2], self.gmv[:, sub, 0:1], -1.0, self.gsc[:, sub, 0:1], ALU.mult, ALU.mult,
                         [self.gmvb, self.gscb], [self.gscb])
                self.act(self.V[:, sub, :], self.V[:, sub, :], AF.Identity, [self.Vb, self.gscb], [self.Vb],
                         bias=self.gsc[:, sub, 1:2], scale=self.gsc[:, sub, 0:1])
            for fc in range(16):
                g = fc // 2
                pg = 2 + fc % 2
                for sub in range(4):
                    self.mm(self.pb[pg][:, sub * 128:(sub + 1) * 128], self.V[:, sub, fc * 128:(fc + 1) * 128],
                            self.wsT[:, g, :], True, True, [self.Vb, self.wsTb], [self.pbb[pg]])
                for sub in range(4):
                    sgi = (fc * 4 + sub) % 2
                    self.stt(self.sg[sgi][:, 0:128], self.pb[pg][:, sub * 128:(sub + 1) * 128], self.gfm[:, 16 + fc:17 + fc],
                             self.R[:, fc, :], ALU.mult, ALU.add, [self.pbb[pg], self.gfmb, self.Rb], [self.sgb[sgi]])
                    self.tt("pool", self.U[:, fc, sub * 128:(sub + 1) * 128], self.U[:, fc, sub * 128:(sub + 1) * 128],
                            self.sg[sgi][:, 0:128], ALU.mult, [self.Ub, self.sgb[sgi]], [self.Ub])
            for (d0, nd) in ((0, 3), (3, 3), (6, 2)):
                slot, sbuf = self.ring_next()
                w_ = slot[:, 0:16 * nd * 128].rearrange("p (k c) -> p k c", k=16)
                for j in range(nd):
                    dc = d0 + j
                    py = 4 + dc % 4
                    for kc in range(16):
                        self.mm(self.pb[py][:, :], w_[:, kc, j * 128:(j + 1) * 128], self.U[:, kc, :],
                                kc == 0, kc == 15, [sbuf, self.Ub], [self.pbb[py]])
                    self.tt("dve", self.X[:, dc, sl], self.X[:, dc, sl], self.pb[py][:, :], ALU.add,
                            [self.pbb[py], xb], [xb])
            self.layernorm(l, 1, self.X[:, :, sl], self.XB[:, :, sl], xb, xbb, last)

    def build_ml(self, l, pass_no):
        nc, P = self.nc, self.P
        p2 = pass_no == 2
        self.common(4)
        xT = self.dram_in("xT", [D, T])
        xh = self.dram_in("xh", [128, 24])
        hasprev = self.dram_in("hasprev", [128, 1])
        w_in = self.dram_in("ml_w_in", [D, 6152]).rearrange("(k p) c -> p k c", p=128)
        b_in = self.dram_in("ml_b_in", [1, 6152])
        mlfm_d = self.dram_in("ml_fm", [128, 112])
        ident_d = self.dram_in("identf", [128, 128])
        utri_d = self.dram_in("utri", [128, 128])
        if p2:
            w_out = self.dram_in("ml_w_out", [2048, D]).rearrange("(k p) c -> p k c", p=128)
            gC = self.dram_in("gC", [NCORES * 128, 4096])
            gN = self.dram_in("gN", [NCORES * 128, 8])
            gL = self.dram_in("gL", [NCORES * 128, 4])
            msel_d = self.dram_in("msel", [128, 8])
            mlt_d = self.dram_in("mlt", [128, 64])
            out = nc.dram_tensor("outT", [D, T], F32, kind="ExternalOutput").ap()
        else:
            stC = nc.dram_tensor("stC", [128, 4096], F32, kind="ExternalOutput").ap()
            stN = nc.dram_tensor("stN", [128, 8], F32, kind="ExternalOutput").ap()
            stL = nc.dram_tensor("stL", [128, 4], F32, kind="ExternalOutput").ap()
        db = self.dram_b

        def mk(name, shape, dt):
            return self.sb(name, shape, dt), P.buf(name)

        Xk, Xkb = mk("Xk", [128, DC, TT], F32)
        XBk, XBkb = mk("XBk", [128, DC, TT], BF16)
        XH, XHb = mk("XH", [128, DC, 3], F32)
        XHB, XHBb = mk("XHB", [128, DC, 3], BF16)
        C, Cb_ = mk("C", [128, 4, 2, 512], F32)
        Cbufs = [P.buf("C%d" % h) for h in range(4)]
        NS, _ = mk("NS", [128, 4, 2], F32)
        Cb, Cbb = mk("Cb", [128, 2, 512], BF16)
        NBf, NBfb = mk("NBf", [128, 2], BF16)
        QT, QTb = mk("QT", [128, 2, TT], BF16)
        KT, KTb = mk("KT", [128, 2, TT], BF16)
        VH, VHb = mk("VH", [128, 4, 512], BF16)
        KP, KPb = mk("KP", [128, TT + 3], F32)
        HALO, HALOb = mk("HALO", [128, 16, 3], F32)
        KK, KKb = mk("KK", [128, 256], BF16)
        EB, EBb = mk("EB", [128, 128], F32)
        LFrep, LFrepb = mk("LFrep", [128, 128], F32)
        identb, identbb = mk("identb", [128, 128], BF16)
        utri, utrib = mk("utri", [128, 128], F32)
        onesF, onesFb = mk("onesF", [128, 128], F32)
        GT, GTb = mk("GT", [128, 4, 8], F32)
        LFt, LFtb = mk("LFt", [128, 4, 4], F32)
        LF, LFb = mk("LF", [128, 4, 4], F32)
        Bc, Bcb = mk("Bc", [128, 4, 4], F32)
        A_, A_b = mk("A_", [128, 4, 4], F32)
        LD, LDb = mk("LD", [128, 4], F32)
        wsv, wsvb = mk("wsv", [128, 1], F32)
        mlfm, mlfmb = mk("mlfm", [128, 112], F32)
        brow, browb = mk("brow", [1, 2048 + 8], BF16)
        wgt, wgtb = mk("wgt", [128, DC, 8], BF16)
        hp, hpb = mk("hp", [128, 1], F32)
        if p2:
            OT, OTb = mk("OT", [128, 4, TT], BF16)
            HT, HTb = mk("HT", [128, 16, TT], BF16)
            PTs, PTsb = mk("PTs", [128, 128], BF16)
            QS, QSb = mk("QS", [128, 2, 128], BF16)
            DTm, DTmb = mk("DTm", [128, 128], F32)
            HF, HFb = mk("HF", [128, 512], F32)
            HN, HNb = mk("HN", [128, 512], BF16)
            rd, rdb = mk("rd", [128, 1], F32)
            st6, st6b = mk("st6", [128, 6], F32)
            mv, mvb = mk("mv", [128, 2], F32)
            rs2, rs2b = mk("rs2", [128, 1], F32)
            STG, STGb = mk("STG", [128, 1024], F32)
            GLs, GLsb = mk("GLs", [128, 8, 4], F32)
            GNs, GNsb = mk("GNs", [128, 8, 8], F32)
            EE, EEb = mk("EE", [128, 8, 4], F32)
            msel, mselb = mk("msel", [128, 8], F32)
            mlt, mltb = mk("mlt", [128, 64], F32)
        pb, pbb = self.pb, self.pbb
        ones, onesb = self.ones, self.onesb

        for tb in range(NTT):
            for h in range(4):
                if p2:
                    self.ring_schedule([(0, DC, 256, w_in[:, :, h * 256:(h + 1) * 256]),
                                        (2048, DC, 256, w_in[:, :, 1024 + h * 256:1024 + (h + 1) * 256])])
                    self.ring_schedule([(0, DC, 512, w_in[:, :, 2048 + h * 512:2048 + (h + 1) * 512])])
                    self.ring_schedule([(0, DC, 512, w_in[:, :, 4096 + h * 512:4096 + (h + 1) * 512])])
                else:
                    self.ring_schedule([(0, DC, 256, w_in[:, :, 1024 + h * 256:1024 + (h + 1) * 256]),
                                        (2048, DC, 512, w_in[:, :, 2048 + h * 512:2048 + (h + 1) * 512])])
            if p2:
                for (d0, nd) in ((0, 3), (3, 3), (6, 2)):
                    self.ring_schedule([(0, 16, nd * 128, w_out[:, :, d0 * 128:(d0 + nd) * 128])])

        self.dma("sp", mlfm[:, :], mlfm_d, [db], [mlfmb])
        self.dma("sp", utri[:, :], utri_d, [db], [utrib])
        self.dma("pool", identb[:, :], ident_d, [db], [identbb])
        self.dma("pool", brow[0:1, 0:2048], b_in[:, 2048:4096], [db], [browb])
        self.dma("pool", brow[0:1, 2048:2056], b_in[:, 6144:6152], [db], [browb])
        self.dma("pool", wgt[:, :, :], w_in[:, :, 6144:6152], [db], [wgtb])
        self.dma("sp", hp[:, :], hasprev, [db], [hpb])
        self.dma("sp", XH[:, :, :], xh.rearrange("p (c t) -> p c t", c=DC), [db], [XHb])
        self.ring_prime()
        P.add("pool", lambda e: e.memset(onesF[:, :], 1.0), [], [onesFb])
        P.add("pool", lambda e: e.memset(LD[:, :], 0.0), [], [LDb])
        self.act(XHB[:, :, :], XH[:, :, :], AF.Copy, [XHb], [XHBb], scale=1.0 / ALPHA)
        allC = [Cb_] + Cbufs
        if not p2:
            P.add("pool", lambda e: e.memset(C[:, :, :, :], 0.0), [], allC)
            P.add("pool", lambda e: e.memset(NS[:, :, :], 0.0), [], allC)
        else:
            self.dma("sp", msel[:, :], msel_d, [db], [mselb])
            self.dma("sp", mlt[:, :], mlt_d, [db], [mltb])
            self.dma("sp", GLs[:, :, :], gL.rearrange("(r p) c -> p r c", p=128), [db], [GLsb])
            self.dma("sp", GNs[:, :, :], gN.rearrange("(r p) c -> p r c", p=128), [db], [GNsb])
            P.add("pool", lambda e: e.memset(EE[:, :, :], 0.0), [], [EEb])
            P.add("pool", lambda e: e.memset(C[:, :, :, :], 0.0), [], allC)
            P.add("pool", lambda e: e.memset(NS[:, :, :], 0.0), [], allC)
            for c1 in range(NCORES):
                for c2 in range(NCORES):
                    if c2 <= c1:
                        continue
                    self.stt(EE[:, c1, :], GLs[:, c2, :], mlt[:, c2 * 8 + c1:c2 * 8 + c1 + 1], EE[:, c1, :],
                             ALU.mult, ALU.add, [GLsb, mltb, EEb], [EEb])
            self.act(EE[:, :, :], EE[:, :, :], AF.Exp, [EEb], [EEb])
            for c1 in range(NCORES - 1):
                self.ts("dve", EE[:, c1, :], EE[:, c1, :], msel[:, c1:c1 + 1], None, ALU.mult, None, [EEb, mselb], [EEb])
                for h in range(4):
                    self.dma("sp", STG[:, :], gC[c1 * 128:(c1 + 1) * 128, h * 1024:(h + 1) * 1024], [db], [STGb])
                    self.stt(C[:, h, :, :], STG[:, :].rearrange("p (a b) -> p a b", a=2), EE[:, c1, h:h + 1], C[:, h, :, :],
                             ALU.mult, ALU.add, [STGb, EEb] + allC, allC)
                    self.stt(NS[:, h, :], GNs[:, c1, h * 2:h * 2 + 2], EE[:, c1, h:h + 1], NS[:, h, :],
                             ALU.mult, ALU.add, [GNsb, EEb] + allC, allC)

        cnt = [0]

        def inproj_fm(slot, sbuf, col0, XBv, xbb_, n):
            pg = cnt[0] % 2
            cnt[0] += 1
            w_ = slot[:, col0[0]:col0[0] + DC * col0[1]].rearrange("p (k c) -> p k c", k=DC)
            for kc in range(DC):
                self.mm(pb[pg][:, 0:n], w_[:, kc, col0[2]:col0[2] + 128], XBv(kc), kc == 0, kc == DC - 1,
                        [sbuf, xbb_], [pbb[pg]])
            return pg

        def qk_chunk(tb, slot, sbuf, off, width, j, fcg, dst, dstb, scale):
            pg = inproj_fm(slot, sbuf, (off, width, j * 128), lambda kc: XBk[:, kc, :], XBkb, TT)
            self.act(KP[:, 3:TT + 3], pb[pg][:, :], AF.Identity, [pbb[pg], mlfmb], [KPb], bias=mlfm[:, fcg:fcg + 1])
            if tb == 0:
                w_ = slot[:, off:off + DC * width].rearrange("p (k c) -> p k c", k=DC)
                for kc in range(DC):
                    self.mm(pb[2][:, 300:303], w_[:, kc, j * 128:(j + 1) * 128], XHB[:, kc, :], kc == 0, kc == DC - 1,
                            [sbuf, XHBb], [pbb[2]])
                self.act(KP[:, 0:3], pb[2][:, 300:303], AF.Identity, [pbb[2], mlfmb], [KPb], bias=mlfm[:, fcg:fcg + 1])
                self.ts("dve", KP[:, 0:3], KP[:, 0:3], hp[:, 0:1], None, ALU.mult, None, [KPb, hpb], [KPb])
            else:
                P.add("pool", lambda e: e.tensor_copy(KP[:, 0:3], HALO[:, fcg, :]), [HALOb], [KPb])
            P.add("pool", lambda e: e.tensor_copy(HALO[:, fcg, :], KP[:, TT:TT + 3]), [KPb], [HALOb])
            acc, accb = self.sg[0], self.sgb[0]
            cw = lambda t: mlfm[:, 32 + fcg * 4 + t:32 + fcg * 4 + t + 1]
            self.ts("dve", acc[:, :], KP[:, 0:TT], cw(0), None, ALU.mult, None, [KPb, mlfmb], [accb])
            for t in range(1, 4):
                self.stt(acc[:, :], KP[:, t:t + TT], cw(t), acc[:, :], ALU.mult, ALU.add, [KPb, mlfmb, accb], [accb])
            if scale is None:
                self.act(dst[:, j, :], acc[:, :], AF.Silu, [accb], [dstb])
            else:
                t2, t2b = self.sg[1], self.sgb[1]
                self.act(t2[:, :], acc[:, :], AF.Silu, [accb], [t2b])
                self.ts("pool", dst[:, j, :], t2[:, :], scale, None, ALU.mult, None, [t2b], [dstb])

        def v_head(slot, sbuf, off, h):
            w_ = slot[:, off:off + DC * 512].rearrange("p (k c) -> p k c", k=DC)
            for sub in range(4):
                pg = cnt[0] % 2
                cnt[0] += 1
                for kc in range(DC):
                    self.mm(pb[pg][:, :], XBk[:, kc, sub * 128:(sub + 1) * 128], w_[:, kc, :], kc == 0, False,
                            [sbuf, XBkb], [pbb[pg]])
                self.mm(pb[pg][:, :], ones[0:1, :], brow[0:1, h * 512:(h + 1) * 512], False, True,
                        [onesb, browb], [pbb[pg]])
                self.act(VH[:, sub, :], pb[pg][:, :], AF.Copy, [pbb[pg]], [VHb])

        def gates_block():
            for sub in range(4):
                for kc in range(DC):
                    self.mm(pb[2][:, sub * 8:(sub + 1) * 8], XBk[:, kc, sub * 128:(sub + 1) * 128], wgt[:, kc, :],
                            kc == 0, False, [wgtb, XBkb], [pbb[2]])
                self.mm(pb[2][:, sub * 8:(sub + 1) * 8], ones[0:1, :], brow[0:1, 2048:2056], False, True,
                        [onesb, browb], [pbb[2]])
            self.act(GT[:, :, :], pb[2][:, 0:32].rearrange("p (a b) -> p a b", a=4), AF.Copy, [pbb[2]], [GTb])
            self.act(LFt[:, :, :], GT[:, :, 4:8], AF.Exp, [GTb], [LFtb], scale=-1.0)
            self.act(LFt[:, :, :], LFt[:, :, :], AF.Ln, [LFtb, onesFb], [LFtb], bias=onesF[:, 0:1])
            self.ts("dve", LF[:, :, :], LFt[:, :, :], -1.0, None, ALU.mult, None, [LFtb], [LFb])
            for sub in range(4):
                self.mm(pb[2][:, 32 + sub * 4:36 + sub * 4], utri[:, :], LF[:, sub, :], True, True, [utrib, LFb], [pbb[2]])
            self.act(Bc[:, :, :], pb[2][:, 32:48].rearrange("p (a b) -> p a b", a=4), AF.Copy, [pbb[2]], [Bcb])
            self.tt("dve", A_[:, :, :], GT[:, :, 0:4], Bc[:, :, :], ALU.subtract, [GTb, Bcb], [A_b])
            if not p2:
                for sub in range(4):
                    self.mm(pb[2][:, 48:52], onesF[:, :], LF[:, sub, :], sub == 0, sub == 3, [onesFb, LFb], [pbb[2]])
                self.tt("dve", LD[:, :], LD[:, :], pb[2][:, 48:52], ALU.add, [pbb[2], LDb], [LDb])

        def gate_mats(sub, h):
            self.ts("dve", LFrep[:, :], onesF[:, :], LF[:, sub, h:h + 1], None, ALU.mult, None, [onesFb, LFb], [LFrepb])
            self.mm(pb[2][:, 128:256], LFrep[:, :], utri[:, :], True, True, [LFrepb, utrib], [pbb[2]])
            self.act(EB[:, :], pb[2][:, 128:256], AF.Exp, [pbb[2]], [EBb])
            self.act(wsv[:, :], pb[2][:, 255:256], AF.Exp, [pbb[2], A_b], [wsvb], bias=A_[:, sub, h:h + 1])
            if p2:
                self.act(DTm[:, :], pb[2][:, 128:256], AF.Exp, [pbb[2], A_b], [DTmb], bias=A_[:, sub, h:h + 1])
                self.tt("pool", DTm[:, :], DTm[:, :], utri[:, :], ALU.mult, [DTmb, utrib], [DTmb])

        def state_update(sub, h):
            ts_ = slice(sub * 128, (sub + 1) * 128)
            for fc in range(2):
                self.mm(pb[5][:, fc * 128:(fc + 1) * 128], KT[:, fc, ts_], identb[:, :], True, True, [KTb, identbb], [pbb[5]])
            self.ts("dve", KK[:, :], pb[5][:, 0:256], wsv[:, 0:1], None, ALU.mult, None, [pbb[5], wsvb], [KKb])
            for fc in range(2):
                self.mm(pb[6 + fc][:, :], KK[:, fc * 128:(fc + 1) * 128], VH[:, sub, :], True, True, [KKb, VHb], [pbb[6 + fc]])
                self.mm(pb[2][:, 304 + fc:305 + fc], KK[:, fc * 128:(fc + 1) * 128], ones[:, 0:1], True, True,
                        [KKb, onesb], [pbb[2]])
            for fc in range(2):
                self.stt(C[:, h, fc, :], C[:, h, fc, :], EB[:, 127:128], pb[6 + fc][:, :], ALU.mult, ALU.add,
                         [Cbufs[h], Cb_, EBb, pbb[6 + fc]], [Cbufs[h]])
            self.stt(NS[:, h, :], NS[:, h, :], EB[:, 127:128], pb[2][:, 304:306], ALU.mult, ALU.add,
                     [Cbufs[h], Cb_, EBb, pbb[2]], [Cbufs[h]])

        def chunk_out(tb, sub, h):
            ts_ = slice(sub * 128, (sub + 1) * 128)
            for fc in range(2):
                self.mm(pb[3][:, 0:128], KT[:, fc, ts_], QT[:, fc, ts_], fc == 0, fc == 1, [KTb, QTb], [pbb[3]])
            self.tt("dve", PTs[:, :], pb[3][:, 0:128], DTm[:, :], ALU.mult, [pbb[3], DTmb], [PTsb])
            for fc in range(2):
                self.tt("pool", QS[:, fc, :], QT[:, fc, ts_], EB[:, :], ALU.mult, [QTb, EBb], [QSb])
            self.act(Cb[:, :, :], C[:, h, :, :], AF.Copy, [Cbufs[h], Cb_], [Cbb])
            self.act(NBf[:, :], NS[:, h, :], AF.Copy, [Cbufs[h], Cb_], [NBfb])
            self.mm(pb[4][:, :], PTs[:, :], VH[:, sub, :], True, False, [PTsb, VHb], [pbb[4]])
            for fc in range(2):
                self.mm(pb[4][:, :], QS[:, fc, :], Cb[:, fc, :], False, fc == 1, [QSb, Cbb], [pbb[4]])
            self.mm(pb[2][:, 310:311], PTs[:, :], ones[:, 0:1], True, False, [PTsb, onesb], [pbb[2]])
            for fc in range(2):
                self.mm(pb[2][:, 310:311], QS[:, fc, :], NBf[:, fc:fc + 1], False, fc == 1, [QSb, NBfb], [pbb[2]])
            self.act(rd[:, :], pb[2][:, 310:311], AF.Abs, [pbb[2]], [rdb])
            self.ts("dve", rd[:, :], rd[:, :], 1.0, None, ALU.max, None, [rdb], [rdb])
            P.add("dve", lambda e: e.reciprocal(rd[:, :], rd[:, :]), [rdb], [rdb])
            self.ts("dve", HF[:, :], pb[4][:, :], rd[:, 0:1], None, ALU.mult, None, [pbb[4], rdb], [HFb])
            P.add("dve", lambda e: e.bn_stats(st6[:, :], HF[:, :]), [HFb], [st6b])
            P.add("dve", lambda e: e.bn_aggr(mv[:, :], st6[:, :]), [st6b], [mvb])
            self.act(rs2[:, :], mv[:, 1:2], AF.Sqrt, [mvb, self.epsbb], [rs2b], bias=self.epsb[:, :])
            P.add("dve", lambda e: e.reciprocal(rs2[:, :], rs2[:, :]), [rs2b], [rs2b])
            self.stt(mv[:, 1:2], mv[:, 0:1], -1.0, rs2[:, 0:1], ALU.mult, ALU.mult, [mvb, rs2b], [mvb])
            self.act(HN[:, :], HF[:, :], AF.Identity, [HFb, mvb, rs2b], [HNb], bias=mv[:, 1:2], scale=rs2[:, 0:1])
            for vc in range(4):
                self.mm(pb[5][:, 256 + (vc % 2) * 128:384 + (vc % 2) * 128], HN[:, vc * 128:(vc + 1) * 128], identb[:, :],
                        True, True, [HNb, identbb], [pbb[5]])
                self.stt(HT[:, h * 4 + vc, ts_], pb[5][:, 256 + (vc % 2) * 128:384 + (vc % 2) * 128],
                         mlfm[:, 96 + h * 4 + vc:97 + h * 4 + vc], OT[:, vc, ts_], ALU.mult, ALU.mult,
                         [pbb[5], mlfmb, OTb], [HTb])

        outb = P.buf("out")
        for tb in range(NTT):
            sl = slice(tb * TT, (tb + 1) * TT)
            self.dma("sp", Xk[:, :, :], xT[:, sl].rearrange("(c p) t -> p c t", p=128), [db], [Xkb])
            self.act(XBk[:, :, :], Xk[:, :, :], AF.Copy, [Xkb], [XBkb], scale=1.0 / ALPHA)
            gates_block()
            for h in range(4):
                if p2:
                    slot, sbuf = self.ring_next()
                    for j in range(2):
                        qk_chunk(tb, slot, sbuf, 0, 256, j, 2 * h + j, QT, QTb, None)
                    for j in range(2):
                        qk_chunk(tb, slot, sbuf, 2048, 256, j, 8 + 2 * h + j, KT, KTb, 1.0 / 16.0)
                    slot, sbuf = self.ring_next()
                    v_head(slot, sbuf, 0, h)
                    slot, sbuf = self.ring_next()
                    for vc in range(4):
                        pg = inproj_fm(slot, sbuf, (0, 512, vc * 128), lambda kc: XBk[:, kc, :], XBkb, TT)
                        self.act(OT[:, vc, :], pb[pg][:, :], AF.Sigmoid, [pbb[pg], mlfmb], [OTb],
                                 bias=mlfm[:, 16 + h * 4 + vc:17 + h * 4 + vc])
                else:
                    slot, sbuf = self.ring_next()
                    for j in range(2):
                        qk_chunk(tb, slot, sbuf, 0, 256, j, 8 + 2 * h + j, KT, KTb, 1.0 / 16.0)
                    v_head(slot, sbuf, 2048, h)
                for sub in range(4):
                    gate_mats(sub, h)
                    if p2:
                        chunk_out(tb, sub, h)
                    state_update(sub, h)
            if p2:
                for (d0, nd) in ((0, 3), (3, 3), (6, 2)):
                    slot, sbuf = self.ring_next()
                    w_ = slot[:, 0:16 * nd * 128].rearrange("p (k c) -> p k c", k=16)
                    for j in range(nd):
                        dc = d0 + j
                        py = 6 + dc % 2
                        for kc in range(16):
                            self.mm(pb[py][:, :], w_[:, kc, j * 128:(j + 1) * 128], HT[:, kc, :], kc == 0, kc == 15,
                                    [sbuf, HTb], [pbb[py]])
                        self.tt("dve", Xk[:, dc, :], Xk[:, dc, :], pb[py][:, :], ALU.add, [pbb[py], Xkb], [Xkb])
                self.layernorm(l, 1, Xk[:, :, :], None, Xkb, None, False)
                self.dma("sp", out[:, sl].rearrange("(c p) t -> p c t", p=128), Xk[:, :, :], [Xkb], [outb], sembuf=outb)
        if not p2:
            self.dma("sp", stC, C[:, :, :, :].rearrange("p a b c -> p (a b c)"), allC, [outb], sembuf=outb)
            self.dma("sp", stN, NS[:, :, :].rearrange("p a b -> p (a b)"), allC, [outb], sembuf=outb)
            self.dma("sp", stL, LD[:, :], [LDb], [outb], sembuf=outb)
        return self.finish([outb])


def _fm(v):
    v = np.asarray(v, np.float32).reshape(-1, 128)
    return np.ascontiguousarray(v.T)


def default_launches():
    return [
        ("st", [(0, "ffn1")]),
        ("ml1", 0), ("ml2", 0),
        ("st", [(0, "ffn2"), (0, "ple"), (1, "ffn1"), (1, "gm"), (1, "ffn2"), (1, "ple"), (2, "ffn1")]),
        ("ml1", 2), ("ml2", 2),
        ("st", [(2, "ffn2"), (2, "ple"), (3, "ffn1"), (3, "gm"), (3, "ffn2"), (3, "ple")]),
    ]


_LAUNCHES = None
_RUNKW = {}
_TIMES = []
_FINAL = True


def _run(nc, in_maps):
    res = run_bass_kernel_spmd(nc, in_maps, core_ids=list(range(NCORES)), **_RUNKW)
    _TIMES.append(getattr(res, "exec_time_ns", None))
    return res.results


def kernel(x, p, ln_g, ln_b, ffn1_wgu, ffn1_wd, ffn2_wgu, ffn2_wd,
           ml_w_in, ml_b_in, ml_conv, ml_norm_g, ml_w_out,
           gm_w_in, gm_b_in, gm_vn_g, gm_vn_b, gm_ws, gm_bs, gm_w_out,
           ple_wp, ple_wg, ple_bg):
    f32 = lambda a: np.ascontiguousarray(np.asarray(a, np.float32))
    x, p = f32(x), f32(p)
    W = {"ffn1_wgu": f32(ffn1_wgu), "ffn1_wd": f32(ffn1_wd), "ffn2_wgu": f32(ffn2_wgu), "ffn2_wd": f32(ffn2_wd),
         "ple_wp": f32(ple_wp), "ple_wg": f32(ple_wg), "gm_w_in": f32(gm_w_in), "gm_w_out": f32(gm_w_out),
         "gm_ws": f32(gm_ws)}
    ml_w_in, ml_b_in, ml_conv, ml_norm_g, ml_w_out = map(f32, (ml_w_in, ml_b_in, ml_conv, ml_norm_g, ml_w_out))
    gm_b_in, gm_vn_g, gm_vn_b, gm_bs = map(f32, (gm_b_in, gm_vn_g, gm_vn_b, gm_bs))
    lnp = np.concatenate([_fm(f32(ln_g).reshape(-1)), _fm(f32(ln_b).reshape(-1))], axis=1)
    plb = _fm(f32(ple_bg).reshape(-1))
    identf = np.eye(128, dtype=np.float32)
    utri = np.triu(np.ones((128, 128), np.float32))
    launches = _LAUNCHES if _LAUNCHES is not None else default_launches()
    Xs = [np.ascontiguousarray(x[0, c * T:(c + 1) * T, :].T) for c in range(NCORES)]
    raw = True
    states = None
    for li, ln_ in enumerate(launches):
        kind = ln_[0]
        final = _FINAL and li == len(launches) - 1
        if kind == "st":
            items = ln_[1]
            b = Builder()
            nc = b.build_stages(items, raw, final)
            shared = {"lnp": lnp, "plb": plb}
            percore = [dict() for _ in range(NCORES)]
            for (l, s) in items:
                j = l // 2
                if s in ("ffn1", "ffn2"):
                    shared["%s_wgu_%d" % (s, l)] = W[s + "_wgu"][l]
                    shared["%s_wd_%d" % (s, l)] = W[s + "_wd"][l]
                elif s == "ple":
                    shared["ple_wp_%d" % l] = W["ple_wp"][l]
                    shared["ple_wg_%d" % l] = W["ple_wg"][l]
                    for c in range(NCORES):
                        percore[c]["pT_%d" % l] = np.ascontiguousarray(p[l, 0, c * T:(c + 1) * T, :].T)
                elif s == "gm":
                    shared["gm_w_in_%d" % l] = W["gm_w_in"][j]
                    shared["gm_w_out_%d" % l] = W["gm_w_out"][j]
                    shared["gm_fm_%d" % l] = np.ascontiguousarray(np.concatenate(
                        [_fm(gm_b_in[j, 0:2048]), _fm(gm_vn_g[j]), _fm(gm_vn_b[j])], axis=1))
                    shared["gm_b_in_%d" % l] = np.ascontiguousarray(gm_b_in[j][None, :])
                    shared["gm_ws_%d" % l] = W["gm_ws"][j]
                    shared["gm_bs_%d" % l] = np.ascontiguousarray(np.broadcast_to(gm_bs[j].reshape(1, -1), (128, 1024)))
                    shared["identf"] = identf
            in_maps = []
            for c in range(NCORES):
                m = dict(shared)
                m.update(percore[c])
                m["xT"] = Xs[c]
                in_maps.append(m)
            res = _run(nc, in_maps)
            Xs = [np.ascontiguousarray(np.asarray(r["outT"], np.float32)) for r in res]
            raw = False
        else:
            l = ln_[1]
            j = l // 2
            pass_no = 1 if kind == "ml1" else 2
            b = Builder()
            nc = b.build_ml(l, pass_no)
            ml_fm = np.ascontiguousarray(np.concatenate([
                _fm(ml_b_in[j, 0:2048]), _fm(ml_b_in[j, 4096:6144]),
                ml_conv[j].reshape(4, 16, 128).transpose(2, 1, 0).reshape(128, 64),
                _fm(ml_norm_g[j])], axis=1))
            shared = {"lnp": lnp, "ml_w_in": ml_w_in[j], "ml_b_in": np.ascontiguousarray(ml_b_in[j][None, :]),
                      "ml_fm": ml_fm, "identf": identf, "utri": utri}
            if pass_no == 2:
                shared["ml_w_out"] = ml_w_out[j]
                shared["gC"] = np.ascontiguousarray(np.concatenate([s_["stC"] for s_ in states], axis=0))
                shared["gN"] = np.ascontiguousarray(np.concatenate([s_["stN"] for s_ in states], axis=0))
                shared["gL"] = np.ascontiguousarray(np.concatenate([s_["stL"] for s_ in states], axis=0))
            in_maps = []
            for c in range(NCORES):
                m = dict(shared)
                m["xT"] = Xs[c]
                if c == 0:
                    xh = np.zeros((128, 24), np.float32)
                else:
                    xh = np.ascontiguousarray(Xs[c - 1][:, T - 3:].reshape(DC, 128, 3).transpose(1, 0, 2).reshape(128, 24))
                m["xh"] = xh
                m["hasprev"] = np.full((128, 1), 0.0 if c == 0 else 1.0, np.float32)
                if pass_no == 2:
                    msel = np.zeros((128, 8), np.float32)
                    mlt = np.zeros((128, 64), np.float32)
                    for c1 in range(NCORES):
                        if c1 < c:
                            msel[:, c1] = 1.0
                        for c2 in range(NCORES):
                            if c1 < c2 < c:
                                mlt[:, c2 * 8 + c1] = 1.0
                    m["msel"], m["mlt"] = msel, mlt
                in_maps.append(m)
            res = _run(nc, in_maps)
            if pass_no == 1:
                states = [{k: np.asarray(r[k], np.float32) for k in ("stC", "stN", "stL")} for r in res]
            else:
                Xs = [np.ascontiguousarray(np.asarray(r["outT"], np.float32)) for r in res]
    out = np.concatenate([X_.T for X_ in Xs], axis=0)[None]
    return np.ascontiguousarray(out.astype(np.float32))
```
